# Optimizing a Trainium2 kernel written in Bass

```python
import math
import jax
import jax.numpy as jnp
from jax import lax
import numpy as np

D_MODEL = 1024
BATCH = 8
SEQ = 4096
DEPTH = 1

HEAD_DIM = 64
N_HEADS_NA = 8
N_HEADS_DIL = 8
N_HEADS = N_HEADS_NA + N_HEADS_DIL
D_NA = N_HEADS_NA * HEAD_DIM
D_DIL = N_HEADS_DIL * HEAD_DIM
D_MIX = D_NA + D_DIL
D_FF = 2816
GRID_W = 64
NA_KH = 8
NA_KW = 16
NA_COL_BLOCK = 16
NA_COL_SPAN = 32
DIL_PATTERNS = ((128, 1), (512, 4), (2048, 16))
DIL_QBLOCK = 128
T5_BUCKETS = 32
T5_MAX_DIST = 1024
NORM_EPS = 1e-6
MASK_VALUE = -1e30

kernel_name = 'hybrid_na_dilated_macaron_layer'


def rmsnorm(x, g):
    xf = x.astype(jnp.float32)
    y = xf * lax.rsqrt(jnp.mean(xf * xf, axis=-1, keepdims=True) + NORM_EPS)
    return (y * g.astype(jnp.float32)).astype(x.dtype)


def swiglu(x, w_gate, w_up, w_down):
    return (jax.nn.silu(x @ w_gate) * (x @ w_up)) @ w_down


def t5_bucket(rel):
    nb = T5_BUCKETS // 2
    max_exact = nb // 2
    n = jnp.abs(rel)
    large = max_exact + (jnp.log(jnp.maximum(n, 1).astype(jnp.float32) / max_exact)
                         / math.log(T5_MAX_DIST / max_exact) * (nb - max_exact)).astype(jnp.int32)
    large = jnp.minimum(large, nb - 1)
    return jnp.where(rel > 0, nb, 0) + jnp.where(n < max_exact, n, large)


def neighbourhood_attention(q, k, v, rel_bias):
    B, H, S, Dh = q.shape
    rows = S // GRID_W
    kh = min(NA_KH, rows)
    ncb = GRID_W // NA_COL_BLOCK
    n_keys = kh * NA_COL_SPAN
    r = jnp.arange(rows)
    key_rows = jnp.clip(r - kh // 2, 0, rows - kh)[:, None] + jnp.arange(kh)[None, :]
    cb = jnp.arange(ncb)
    key_cols = (jnp.clip(cb * NA_COL_BLOCK - NA_KW // 2, 0, GRID_W - NA_COL_SPAN)[:, None]
                + jnp.arange(NA_COL_SPAN)[None, :])
    key_idx = (key_rows[:, None, :, None] * GRID_W
               + key_cols[None, :, None, :]).reshape(rows, ncb, n_keys)
    kr = jnp.repeat(key_rows, NA_COL_SPAN, axis=1)
    kc = jnp.tile(key_cols, (1, kh))
    q_col = cb[:, None] * NA_COL_BLOCK + jnp.arange(NA_COL_BLOCK)[None, :]
    q_col_start = jnp.clip(q_col - NA_KW // 2, 0, GRID_W - NA_KW)
    col_valid = ((kc[:, None, :] >= q_col_start[:, :, None])
                 & (kc[:, None, :] < q_col_start[:, :, None] + NA_KW))
    dc_idx = jnp.clip(kc[:, None, :] - q_col[:, :, None] + NA_KW - 1, 0, 2 * NA_KW - 2)
    dr_idx = kr - r[:, None] + NA_KH - 1
    scale = HEAD_DIM ** -0.5
    q_rows = jnp.moveaxis(q.reshape(B, H, rows, ncb, NA_COL_BLOCK, Dh), 2, 0)

    def row_block(args):
        q_r, idx_r, dr_r = args
        k_r = k[:, :, idx_r]
        v_r = v[:, :, idx_r]
        s = jnp.einsum('bhnqe,bhnke->bhnqk', q_r, k_r,
                       preferred_element_type=jnp.float32) * scale
        bias = rel_bias[:, dr_r[None, None, :], dc_idx].astype(jnp.float32)
        s = jnp.where(col_valid, s + bias, MASK_VALUE)
        p = jax.nn.softmax(s, axis=-1)
        return jnp.einsum('bhnqk,bhnke->bhnqe', p.astype(v.dtype), v_r)

    out = lax.map(row_block, (q_rows, key_idx, dr_idx))
    return jnp.moveaxis(out, 0, 2).reshape(B, H, S, Dh)


def dilated_window_attention(q, k, v, t5_bias, window, dilation):
    B, H, S, Dh = q.shape
    radius = window // (2 * dilation)
    L = S // dilation
    qb = min(DIL_QBLOCK, L)
    nb = -(-L // qb)
    Lp = nb * qb
    kb_len = qb + 2 * radius

    def to_sub(t):
        return t.reshape(B, H, L, dilation, Dh).transpose(0, 1, 3, 2, 4)

    qs = jnp.pad(to_sub(q), ((0, 0), (0, 0), (0, 0), (0, Lp - L), (0, 0))).reshape(B, H, dilation, nb, qb, Dh)
    pad_k = ((0, 0), (0, 0), (0, 0), (radius, radius + Lp - L), (0, 0))
    ks = jnp.pad(to_sub(k), pad_k)
    vs = jnp.pad(to_sub(v), pad_k)
    k_idx = jnp.arange(nb)[:, None] * qb + jnp.arange(kb_len)[None, :]
    k_blk = ks[:, :, :, k_idx]
    v_blk = vs[:, :, :, k_idx]
    q_pos = jnp.arange(nb)[:, None] * qb + jnp.arange(qb)[None, :]
    k_pos = k_idx - radius
    off = k_pos[:, None, :] - q_pos[:, :, None]
    valid = (jnp.abs(off) <= radius) & (k_pos[:, None, :] >= 0) & (k_pos[:, None, :] < L)
    bias = t5_bias[:, t5_bucket(off * dilation)].astype(jnp.float32)
    s = jnp.einsum('bhdnqe,bhdnke->bhdnqk', qs, k_blk,
                   preferred_element_type=jnp.float32) * HEAD_DIM ** -0.5
    s = jnp.where(valid, s + bias[:, None], MASK_VALUE)
    m = jnp.max(s, axis=-1, keepdims=True)
    p = jnp.exp(s - m)
    den = jnp.sum(p, axis=-1)
    o = jnp.einsum('bhdnqk,bhdnke->bhdnqe', p.astype(v.dtype), v_blk,
                   preferred_element_type=jnp.float32) / den[..., None]
    lse = m[..., 0] + jnp.log(den)
    o = o.reshape(B, H, dilation, Lp, Dh)[:, :, :, :L].transpose(0, 1, 3, 2, 4).reshape(B, H, S, Dh)
    lse = lse.reshape(B, H, dilation, Lp)[..., :L].transpose(0, 1, 3, 2).reshape(B, H, S)
    return o, lse


def dilated_mixture_attention(q, k, v, t5_bias):
    results = [dilated_window_attention(q, k, v, t5_bias, w, d) for w, d in DIL_PATTERNS]
    outs = jnp.stack([res[0] for res in results])
    lses = jnp.stack([res[1] for res in results])
    wts = jax.nn.softmax(lses, axis=0)
    return jnp.sum(wts[..., None] * outs, axis=0).astype(q.dtype)


def setup_inputs(seed: int = 0) -> dict:
    key = jax.random.key(seed)
    ks = jax.random.split(key, 20)

    def nrm(k, shape, scale):
        return jax.random.normal(k, shape, jnp.float32) * scale

    def gain(k, shape):
        return 1.0 + 0.05 * jax.random.normal(k, shape, jnp.float32)

    return {
        'x': nrm(ks[0], (BATCH, SEQ, D_MODEL), 1.0),
        'ffn1_pre_g': gain(ks[1], (DEPTH, D_MODEL)),
        'ffn1_w_gate': nrm(ks[2], (DEPTH, D_MODEL, D_FF), D_MODEL ** -0.5),
        'ffn1_w_up': nrm(ks[3], (DEPTH, D_MODEL, D_FF), D_MODEL ** -0.5),
        'ffn1_w_down': nrm(ks[4], (DEPTH, D_FF, D_MODEL), D_FF ** -0.5),
        'ffn1_post_g': gain(ks[5], (DEPTH, D_MODEL)),
        'mix_pre_g': gain(ks[6], (DEPTH, D_MODEL)),
        'w_qkv': nrm(ks[7], (DEPTH, D_MODEL, 3 * D_MIX), D_MODEL ** -0.5),
        'na_rel_bias': nrm(ks[8], (DEPTH, N_HEADS_NA, 2 * NA_KH - 1, 2 * NA_KW - 1), 0.5),
        't5_rel_bias': nrm(ks[9], (N_HEADS_DIL, T5_BUCKETS), 0.5),
        'na_out_g': gain(ks[10], (DEPTH, D_NA)),
        'dil_out_g': gain(ks[11], (DEPTH, D_DIL)),
        'w_out': nrm(ks[12], (DEPTH, D_MIX, D_MODEL), D_MIX ** -0.5),
        'mix_post_g': gain(ks[13], (DEPTH, D_MODEL)),
        'ffn2_pre_g': gain(ks[14], (DEPTH, D_MODEL)),
        'ffn2_w_gate': nrm(ks[15], (DEPTH, D_MODEL, D_FF), D_MODEL ** -0.5),
        'ffn2_w_up': nrm(ks[16], (DEPTH, D_MODEL, D_FF), D_MODEL ** -0.5),
        'ffn2_w_down': nrm(ks[17], (DEPTH, D_FF, D_MODEL), D_FF ** -0.5),
        'ffn2_post_g': gain(ks[18], (DEPTH, D_MODEL)),
    }


def reference(x, ffn1_pre_g, ffn1_w_gate, ffn1_w_up, ffn1_w_down, ffn1_post_g,
              mix_pre_g, w_qkv, na_rel_bias, t5_rel_bias, na_out_g, dil_out_g, w_out,
              mix_post_g, ffn2_pre_g, ffn2_w_gate, ffn2_w_up, ffn2_w_down, ffn2_post_g):
    B, S, _ = x.shape
    for l in range(DEPTH):
        h = swiglu(rmsnorm(x, ffn1_pre_g[l]), ffn1_w_gate[l], ffn1_w_up[l], ffn1_w_down[l])
        x = x + 0.5 * rmsnorm(h, ffn1_post_g[l])
        h = rmsnorm(x, mix_pre_g[l])
        qkv = (h @ w_qkv[l]).reshape(B, S, 3, N_HEADS, HEAD_DIM).transpose(2, 0, 3, 1, 4)
        q, k, v = qkv[0], qkv[1], qkv[2]
        o_na = neighbourhood_attention(q[:, :N_HEADS_NA], k[:, :N_HEADS_NA], v[:, :N_HEADS_NA],
                                       na_rel_bias[l])
        o_dil = dilated_mixture_attention(q[:, N_HEADS_NA:], k[:, N_HEADS_NA:], v[:, N_HEADS_NA:],
                                          t5_rel_bias)
        o_na = rmsnorm(o_na.transpose(0, 2, 1, 3).reshape(B, S, D_NA), na_out_g[l])
        o_dil = rmsnorm(o_dil.transpose(0, 2, 1, 3).reshape(B, S, D_DIL), dil_out_g[l])
        mixed = jnp.concatenate([o_na, o_dil], axis=-1) @ w_out[l]
        x = x + rmsnorm(mixed, mix_post_g[l])
        h = swiglu(rmsnorm(x, ffn2_pre_g[l]), ffn2_w_gate[l], ffn2_w_up[l], ffn2_w_down[l])
        x = x + 0.5 * rmsnorm(h, ffn2_post_g[l])
    return x
```

```python
import math
from contextlib import ExitStack

import numpy as np
import concourse.bass as bass
import concourse.mybir as mybir
from concourse.bass_utils import run_bass_kernel_spmd

F32 = mybir.dt.float32
BF16 = mybir.dt.bfloat16
U8 = mybir.dt.uint8
AF = mybir.ActivationFunctionType
ALU = mybir.AluOpType

S = 4096
D = 1024
FF = 2816
NCH = FF // 128
NT = 8
TT = 512
EPS = 1e-6
NEG = -30000.0
NDS = 24
DBG = {"nblocks": None, "stage": 5}
CG = [(0, 6), (6, 12), (12, 17), (17, 22)]
DIL = ((128, 1), (512, 4), (2048, 16))


class Buf:
    __slots__ = ("w", "r")

    def __init__(self):
        self.w = None
        self.r = {}


class Sched:
    ENG = ("pe", "act", "dve", "pool", "sp")

    def __init__(self, nc, stack):
        self.nc = nc
        self.semobj = {}
        self.cnt = {}
        self.stream = {}
        self.waited = {}
        for e in self.ENG:
            self.semobj[e] = stack.enter_context(nc.semaphore("s_" + e))
            self.cnt[e] = 0
            self.stream[e] = []
            self.waited[e] = {}
        self.dn = {"sp": 0, "pool": 0}
        for q in ("sp", "pool"):
            for i in range(NDS):
                self.semobj[f"d{q}{i}"] = stack.enter_context(nc.semaphore(f"d{q}{i}"))
        self.out_toks = []
        self.bar = []

    def _all_dma_toks(self):
        toks = []
        for q in ("sp", "pool"):
            n = self.dn[q]
            for k in range(min(n, NDS)):
                uses = (n - 1 - k) // NDS + 1
                toks.append((f"d{q}{k}", 16 * uses))
        return toks

    def barrier(self):
        self.bar = [(e, self.cnt[e]) for e in self.ENG if self.cnt[e] > 0] + self._all_dma_toks()

    def _waits(self, eng, toks):
        out = []
        for t in toks:
            if t is None:
                continue
            key, val = t
            if eng == "pe" and key == "pe":
                continue
            if self.waited[eng].get(key, 0) >= val:
                continue
            self.waited[eng][key] = val
            out.append((self.semobj[key], val))
        return out

    @staticmethod
    def _collect(reads, writes, deps):
        toks = list(deps)
        for b in reads:
            toks.append(b.w)
        for b in writes:
            toks.append(b.w)
            toks.extend(b.r.items())
        return toks

    def op(self, eng, fn, reads=(), writes=(), deps=()):
        w = self._waits(eng, self._collect(reads, writes, deps) + self.bar)
        self.cnt[eng] += 1
        tok = (eng, self.cnt[eng])
        self.stream[eng].append((w, fn, self.semobj[eng], 1))
        for b in reads:
            if b.r.get(eng, 0) < tok[1]:
                b.r[eng] = tok[1]
        for b in writes:
            b.w = tok
            b.r = {}
        return tok

    def dma(self, q, out, in_, reads=(), writes=(), deps=(), is_output=False):
        i = self.dn[q]
        self.dn[q] += 1
        k = i % NDS
        val = 16 * (i // NDS + 1)
        key = f"d{q}{k}"
        toks = self._collect(reads, writes, deps) + self.bar
        if val > 16:
            toks.append((key, val - 16))
        w = self._waits(q, toks)
        self.stream[q].append((w, (lambda e, o=out, i_=in_: e.dma_start(out=o, in_=i_)),
                               self.semobj[key], 16))
        tok = (key, val)
        for b in reads:
            b.r[key] = val
        for b in writes:
            b.w = tok
            b.r = {}
        if is_output:
            self.out_toks.append(tok)
        return tok

    def finish(self):
        toks = list(self.out_toks) + self._all_dma_toks()
        w = self._waits("sp", toks)
        self.stream["sp"].append((w, None, None, 0))

    def emit(self, block):
        def mk(e):
            def f(eng):
                for (w, fn, sem, inc) in self.stream[e]:
                    for (s_, v) in w:
                        eng.wait_ge(s_, v)
                    if fn is None:
                        continue
                    ins = fn(eng)
                    ins.then_inc(sem, inc)
            return f
        block.tensor(mk("pe"))
        block.scalar(mk("act"))
        block.vector(mk("dve"))
        block.gpsimd(mk("pool"))
        block.sync(mk("sp"))


class Carver:
    def __init__(self, big, limit):
        self.big = big
        self.off = 0
        self.limit = limit

    def reset(self, off=0):
        self.off = off

    def take(self, dt, shape_free):
        n = 1
        for s_ in shape_free:
            n *= s_
        esz = 4 if dt == F32 else 2
        nbytes = (n * esz + 63) // 64 * 64
        assert self.off + nbytes <= self.limit, ("SBUF overflow", self.off + nbytes, self.limit)
        ap = self.big[:, self.off:self.off + n * esz].bitcast(dt)
        self.off += nbytes
        if len(shape_free) == 2:
            ap = ap.rearrange("p (a b) -> p a b", b=shape_free[1])
        elif len(shape_free) == 3:
            ap = ap.rearrange("p (a b c) -> p a b c", b=shape_free[1], c=shape_free[2])
        elif len(shape_free) == 4:
            ap = ap.rearrange("p (a b c d) -> p a b c d", b=shape_free[1], c=shape_free[2],
                              d=shape_free[3])
        return ap


def _na_cases():
    cases = [(10, 10 + dl) for dl in (-2, -1, 0, 1, 2)]
    for m in (0, 1):
        cases += [(m, j) for j in range(0, 4)]
    for m in (30, 31):
        cases += [(m, j) for j in range(28, 32)]
    return cases


def _na_chunks(m):
    rows = []
    for r in (2 * m, 2 * m + 1):
        k0 = min(max(r - 4, 0), 56)
        rows += list(range(k0, k0 + 8))
    lo, hi = min(rows), max(rows)
    return list(range(lo // 2, hi // 2 + 1))


def _na_case_index(m, j):
    if m == 0:
        return 5 + j
    if m == 1:
        return 9 + j
    if m == 30:
        return 13 + (j - 28)
    if m == 31:
        return 17 + (j - 28)
    return (j - m) + 2


def _na_tables():
    cases = _na_cases()
    dr = np.zeros((21, 128, 128), np.int64)
    dc = np.zeros((21, 128, 128), np.int64)
    valid = np.zeros((21, 128, 128), bool)
    i = np.arange(128)[:, None]
    q = np.arange(128)[None, :]
    for ci, (m, j) in enumerate(cases):
        rho = 2 * j + i // 64
        kc = i % 64
        r = 2 * m + q // 64
        qc = q % 64
        kr0 = np.clip(r - 4, 0, 56)
        qcs = np.clip(qc - 8, 0, 48)
        v = (rho >= kr0) & (rho < kr0 + 8) & (kc >= qcs) & (kc < qcs + 16)
        valid[ci] = v
        dr[ci] = np.clip(rho - r + 7, 0, 14)
        dc[ci] = np.clip(kc - qc + 15, 0, 30)
    return dr, dc, valid


def _t5_bucket(rel):
    nb = 16
    max_exact = 8
    n = np.abs(rel)
    large = max_exact + (np.log(np.maximum(n, 1).astype(np.float32) / np.float32(max_exact))
                         / np.float32(math.log(1024 / max_exact)) * np.float32(nb - max_exact)
                         ).astype(np.int32)
    large = np.minimum(large, nb - 1)
    return np.where(rel > 0, nb, 0) + np.where(n < max_exact, n, large)


def _dil_tables():
    i = np.arange(128)[:, None]
    q = np.arange(128)[None, :]
    off = np.zeros((128, 384), np.int64)
    valid = np.zeros((128, 384), bool)
    off[:, 0:128] = i - 64 - q
    valid[:, 0:128] = np.abs(i - 64 - q) <= 64
    off[:, 128:256] = 64 + i - q
    valid[:, 128:256] = np.abs(64 + i - q) <= 64
    off[:, 256:384] = i - q
    valid[:, 256:384] = (np.abs(i - q) <= 64) & (i < 64)
    bucket = np.zeros((3, 128, 384), np.int64)
    for p, (_, d) in enumerate(DIL):
        bucket[p] = _t5_bucket(np.clip(off, -64, 64) * d)
    return bucket, valid


def build_program(debug=False, stop_after=None, only=None, b_pairs=None):
    nc = bass.Bass("TRN2", target_bir_lowering=False)

    def din(name, shape, dt=F32):
        return nc.dram_tensor(name, shape, dt, kind="ExternalInput").ap()

    def dscr(name, shape, dt):
        if debug:
            return nc.dram_tensor(name, shape, dt, kind="ExternalOutput").ap()
        return nc.dram_tensor(name, shape, dt).ap()

    x_in = din("x", [S, D])
    gall = din("gall", [8, D])
    w1g = din("w1g", [D, FF]); w1u = din("w1u", [D, FF]); w1d = din("w1d", [FF, D])
    w2g = din("w2g", [D, FF]); w2u = din("w2u", [D, FF]); w2d = din("w2d", [FF, D])
    wqkv = din("wqkv", [D, 3 * D]); wo = din("wo", [D, D])
    nab = din("nab", [128, 8, 21, 128]); nam = din("nam", [128, 21, 128])
    dlb = din("dlb", [128, 8, 3, 384]); dlm = din("dlm", [128, 384])
    out = nc.dram_tensor("out", [S, D], F32, kind="ExternalOutput").ap()
    x1 = din("x1", [S, D]) if only in ("C1", "A2") else dscr("x1", [S, D], F32)
    x2 = dscr("x2", [S, D], F32)
    if only == "B":
        qT = din("qT", [D, S], BF16); kT = din("kT", [D, S], BF16); vv = din("vv", [S, D], BF16)
    else:
        qT = dscr("qT", [D, S], BF16)
        kT = dscr("kT", [D, S], BF16)
        vv = dscr("vv", [S, D], BF16)
    oacc = din("oacc", [4, S, 528]) if only == "C1" else dscr("oacc", [4, S, 528], F32)

    SB_BYTES = 206 * 1024
    with ExitStack() as stack:
        big = stack.enter_context(nc.sbuf_tensor("big", [128, SB_BYTES], U8))
        psum = stack.enter_context(nc.psum_tensor("psum", [128, 4096], F32))
        K = Sched(nc, stack)
        block = stack.enter_context(nc.Block())
        cv = Carver(big, SB_BYTES)

        ident = cv.take(BF16, [128])
        identf = cv.take(F32, [128])
        stat = cv.take(F32, [512])
        b_ident = Buf()
        stat_bufs = [Buf() for _ in range(512)]
        stat_next = [0]

        def newstat(n=1):
            c = stat_next[0]
            if c + n > 512:
                c = 0
            stat_next[0] = c + n
            return c, stat_bufs[c:c + n]

        def mk_ident(e):
            return e.affine_select(out=identf, in_=identf, pattern=[[-1, 128]],
                                   compare_op=ALU.not_equal, fill=1.0, base=0,
                                   channel_multiplier=1)
        b_identf = Buf()
        K.op("pool", lambda e: e.memset(identf, 0.0), writes=[b_identf])
        K.op("pool", mk_ident, writes=[b_identf])
        K.op("dve", lambda e: e.tensor_copy(out=ident, in_=identf), reads=[b_identf],
             writes=[b_ident])
        PERSIST = cv.off

        def bank(i, n=1):
            return psum[:, i * 512:(i + n) * 512]

        def load_gain(row, dst, b_dst, scale=None):
            K.dma("sp", dst, gall[row:row + 1, :].broadcast_to([128, D]), writes=[b_dst])
            if scale is not None:
                K.op("dve", lambda e: e.tensor_scalar(out=dst, in0=dst, scalar1=scale,
                                                      scalar2=None, op0=ALU.mult),
                     reads=[b_dst], writes=[b_dst])

        class Norm:
            def __init__(self, src, g_bc, b_g, hT, b_hT, tp_banks, b_tp):
                self.src, self.g_bc, self.b_g = src, g_bc, b_g
                self.hT, self.b_hT = hT, b_hT
                self.tp_banks, self.b_tp = tp_banks, b_tp
                self.xa = [cv.take(F32, [D]) for _ in range(2)]
                self.b_xa = [Buf(), Buf()]
                self.hb = [cv.take(BF16, [D]) for _ in range(4)]
                self.b_hb = [Buf() for _ in range(4)]
                self.n = 0
                self.pending = []

            def compute(self, t):
                for s_ in range(4):
                    i = self.n
                    self.n += 1
                    sl = i % 2
                    r0 = t * TT + s_ * 128
                    xa, hb = self.xa[sl], self.hb[s_]
                    bhb = self.b_hb[s_]
                    K.dma("sp", xa, self.src[r0:r0 + 128, :], writes=[self.b_xa[sl]])
                    c, sb = newstat(3)
                    K.op("act", lambda e, xa=xa, hb=hb, c=c: e.activation(
                        out=hb, in_=xa, func=AF.Square, accum_out=stat[:, c:c + 1]),
                        reads=[self.b_xa[sl]], writes=[bhb, sb[0]])
                    K.op("act", lambda e, c=c: e.activation(
                        out=stat[:, c + 1:c + 2], in_=stat[:, c:c + 1], func=AF.Sqrt,
                        scale=1.0 / D, bias=EPS), reads=[sb[0]], writes=[sb[1]])
                    K.op("dve", lambda e, c=c: e.reciprocal(out=stat[:, c + 2:c + 3],
                                                            in_=stat[:, c + 1:c + 2]),
                         reads=[sb[1]], writes=[sb[2]])
                    K.op("dve", lambda e, xa=xa, hb=hb, c=c: e.scalar_tensor_tensor(
                        out=hb, in0=xa, scalar=stat[:, c + 2:c + 3], in1=self.g_bc,
                        op0=ALU.mult, op1=ALU.mult),
                        reads=[self.b_xa[sl], sb[2], self.b_g], writes=[bhb])
                    self.pending.append((t, s_, s_))

            def transpose(self, hslot):
                for (t, s_, sl) in self.pending:
                    tp = self.tp_banks[s_ % len(self.tp_banks)]
                    btp = self.b_tp[s_ % len(self.tp_banks)]
                    hb = self.hb[sl]

                    def mm(e, hb=hb, tp=tp):
                        ins = None
                        for k in range(8):
                            ins = e.matmul(tp[:, k * 128:(k + 1) * 128],
                                           lhsT=hb[:, k * 128:(k + 1) * 128], rhs=ident,
                                           start=True, stop=True)
                        return ins
                    K.op("pe", mm, reads=[self.b_hb[sl], b_ident], writes=[btp])
                    dst = self.hT[hslot][:, :, s_ * 128:(s_ + 1) * 128]
                    K.op("dve", lambda e, dst=dst, tp=tp: e.tensor_copy(
                        out=dst, in_=tp.rearrange("p (k q) -> p k q", q=128)),
                        reads=[btp], writes=[self.b_hT[hslot]])
                self.pending = []

        class Post:
            def __init__(self, res_src, dst, g_bc, b_g, is_output):
                self.res_src, self.dst, self.g_bc, self.b_g = res_src, dst, g_bc, b_g
                self.is_output = is_output
                self.xr = [cv.take(F32, [D]) for _ in range(2)]
                self.b_xr = [Buf(), Buf()]
                self.tmp = cv.take(F32, [D])
                self.b_tmp = Buf()
                self.n = 0

            def run(self, r0, dps, b_dps):
                sl = self.n % 2
                self.n += 1
                xr = self.xr[sl]
                K.dma("sp", xr, self.res_src[r0:r0 + 128, :], writes=[self.b_xr[sl]])
                c, sb = newstat(3)
                tmp = self.tmp
                K.op("act", lambda e, c=c: e.activation(
                    out=tmp, in_=dps, func=AF.Square, accum_out=stat[:, c:c + 1]),
                    reads=[b_dps], writes=[self.b_tmp, sb[0]])
                K.op("act", lambda e, c=c: e.activation(
                    out=stat[:, c + 1:c + 2], in_=stat[:, c:c + 1], func=AF.Sqrt,
                    scale=1.0 / D, bias=EPS), reads=[sb[0]], writes=[sb[1]])
                K.op("dve", lambda e, c=c: e.reciprocal(out=stat[:, c + 2:c + 3],
                                                        in_=stat[:, c + 1:c + 2]),
                     reads=[sb[1]], writes=[sb[2]])
                K.op("dve", lambda e, c=c: e.scalar_tensor_tensor(
                    out=tmp, in0=dps, scalar=stat[:, c + 2:c + 3], in1=self.g_bc,
                    op0=ALU.mult, op1=ALU.mult),
                    reads=[b_dps, sb[2], self.b_g], writes=[self.b_tmp])
                K.op("pool", lambda e, xr=xr: e.tensor_tensor(out=xr, in0=tmp, in1=xr,
                                                              op=ALU.add),
                     reads=[self.b_tmp], writes=[self.b_xr[sl]])
                K.dma("sp", self.dst[r0:r0 + 128, :], xr, reads=[self.b_xr[sl]],
                      is_output=self.is_output)

        def ffn_phase(src, dst, wg, wu, wd, row_pre, row_post, is_output):
            cv.reset(PERSIST)
            Wg = cv.take(BF16, [8, FF]); Wu = cv.take(BF16, [8, FF]); Wd = cv.take(BF16, [NCH, D])
            b_wg = [Buf() for _ in CG]; b_wu = [Buf() for _ in CG]; b_wd = [Buf() for _ in CG]
            gpre = cv.take(F32, [D]); gpost = cv.take(F32, [D])
            b_gpre, b_gpost = Buf(), Buf()
            hT = [cv.take(BF16, [8, TT])]
            b_hT = [Buf()]
            act = cv.take(BF16, [NCH, TT])
            b_act = [Buf() for _ in range(NCH)]
            sil = [cv.take(BF16, [TT]) for _ in range(2)]
            b_sil = [Buf(), Buf()]
            gu = [(bank(0), bank(1)), (bank(2), bank(3))]
            b_gu = [Buf(), Buf()]
            tpb = [bank(0, 2), bank(2, 2)]
            dpair = [bank(4, 2), bank(6, 2)]
            b_dp = [Buf(), Buf()]
            load_gain(row_pre, gpre, b_gpre)
            load_gain(row_post, gpost, b_gpost, scale=0.5)
            nrm = Norm(src, gpre, b_gpre, hT, b_hT, tpb, b_gu)
            post = Post(src, dst, gpost, b_gpost, is_output)
            wgv = wg.rearrange("(k p) f -> p k f", p=128)
            wuv = wu.rearrange("(k p) f -> p k f", p=128)
            wdv = wd.rearrange("(c p) d -> p c d", p=128)
            for gi, (c0, c1) in enumerate(CG):
                K.dma("pool", Wg[:, :, c0 * 128:c1 * 128], wgv[:, :, c0 * 128:c1 * 128],
                      writes=[b_wg[gi]])
                K.dma("pool", Wu[:, :, c0 * 128:c1 * 128], wuv[:, :, c0 * 128:c1 * 128],
                      writes=[b_wu[gi]])
            for gi, (c0, c1) in enumerate(CG):
                K.dma("pool", Wd[:, c0:c1, :], wdv[:, c0:c1, :], writes=[b_wd[gi]])

            def grp(c):
                for gi, (c0, c1) in enumerate(CG):
                    if c0 <= c < c1:
                        return gi

            nrm.compute(0)
            nrm.transpose(0)
            ncnt = [0]
            for t in range(NT):
                for c in range(NCH):
                    sl = ncnt[0] % 2
                    ncnt[0] += 1
                    G, U = gu[sl]

                    def mm(e, c=c, G=G, U=U):
                        ins = None
                        for k in range(8):
                            ins = e.matmul(G, lhsT=Wg[:, k, c * 128:(c + 1) * 128],
                                           rhs=hT[0][:, k, :], start=(k == 0), stop=(k == 7))
                        for k in range(8):
                            ins = e.matmul(U, lhsT=Wu[:, k, c * 128:(c + 1) * 128],
                                           rhs=hT[0][:, k, :], start=(k == 0), stop=(k == 7))
                        return ins
                    K.op("pe", mm, reads=[b_wg[grp(c)], b_wu[grp(c)], b_hT[0]],
                         writes=[b_gu[sl]])
                    K.op("act", lambda e, G=G, sl=sl: e.activation(out=sil[sl], in_=G,
                                                                   func=AF.Silu),
                         reads=[b_gu[sl]], writes=[b_sil[sl]])
                    K.op("dve", lambda e, U=U, sl=sl, c=c: e.tensor_tensor(
                        out=act[:, c, :], in0=sil[sl], in1=U, op=ALU.mult),
                        reads=[b_gu[sl], b_sil[sl]], writes=[b_act[c]])
                if t + 1 < NT:
                    nrm.compute(t + 1)
                for s_ in range(4):
                    if s_ == 2 and t + 1 < NT:
                        nrm.transpose(0)
                    dsl = s_ % 2
                    dps = dpair[dsl]

                    def mmd(e, s_=s_, dps=dps):
                        ins = None
                        for half in range(2):
                            for c in range(NCH):
                                ins = e.matmul(dps[:, half * 512:(half + 1) * 512],
                                               lhsT=act[:, c, s_ * 128:(s_ + 1) * 128],
                                               rhs=Wd[:, c, half * 512:(half + 1) * 512],
                                               start=(c == 0), stop=(c == NCH - 1))
                        return ins
                    K.op("pe", mmd, reads=b_act + b_wd, writes=[b_dp[dsl]])
                    post.run(t * TT + s_ * 128, dps, b_dp[dsl])

        def qkv_phase():
            cv.reset(PERSIST)
            W = cv.take(BF16, [8, 3 * D])
            b_w = [Buf() for _ in range(6)]
            gpre = cv.take(F32, [D]); b_gpre = Buf()
            hT = [cv.take(BF16, [8, TT]) for _ in range(2)]
            b_hT = [Buf(), Buf()]
            stg = [cv.take(BF16, [TT]) for _ in range(4)]
            b_stg = [Buf() for _ in range(4)]
            vst = [cv.take(BF16, [D]) for _ in range(2)]
            b_vst = [Buf(), Buf()]
            banks = [bank(i) for i in range(4)]
            b_bk = [Buf() for _ in range(4)]
            tpb = [bank(4, 2), bank(6, 2)]
            b_tp = [Buf(), Buf()]
            load_gain(2, gpre, b_gpre)
            wv_ = wqkv.rearrange("(k p) f -> p k f", p=128)
            for gi in range(6):
                K.dma("pool", W[:, :, gi * 512:(gi + 1) * 512], wv_[:, :, gi * 512:(gi + 1) * 512],
                      writes=[b_w[gi]])
            nrm = Norm(x1, gpre, b_gpre, hT, b_hT, tpb, b_tp)
            nrm.compute(0)
            nrm.transpose(0)
            n = [0]
            for t in range(NT):
                hs = t % 2
                if t + 1 < NT:
                    nrm.compute(t + 1)
                for c in range(16):
                    i = n[0]; n[0] += 1
                    bk, bb = banks[i % 4], b_bk[i % 4]

                    def mm(e, c=c, bk=bk, hs=hs):
                        ins = None
                        for k in range(8):
                            ins = e.matmul(bk, lhsT=W[:, k, c * 128:(c + 1) * 128],
                                           rhs=hT[hs][:, k, :], start=(k == 0), stop=(k == 7))
                        return ins
                    K.op("pe", mm, reads=[b_w[c // 4], b_hT[hs]], writes=[bb])
                    st, bs = stg[i % 4], b_stg[i % 4]
                    sc = 0.125 if c < 8 else 1.0
                    if i % 2 == 0:
                        K.op("act", lambda e, st=st, bk=bk, sc=sc: e.activation(
                            out=st, in_=bk, func=AF.Copy, scale=sc), reads=[bb], writes=[bs])
                    else:
                        K.op("dve", lambda e, st=st, bk=bk, sc=sc: e.tensor_scalar(
                            out=st, in0=bk, scalar1=sc, scalar2=None, op0=ALU.mult),
                            reads=[bb], writes=[bs])
                    dstT = qT if c < 8 else kT
                    cc = c % 8
                    K.dma("sp", dstT[cc * 128:(cc + 1) * 128, t * TT:(t + 1) * TT], st, reads=[bs])
                for s_ in range(4):
                    vs, bv = vst[s_ % 2], b_vst[s_ % 2]
                    for half in range(2):
                        i = n[0]; n[0] += 1
                        bk, bb = banks[i % 4], b_bk[i % 4]

                        def mmv(e, s_=s_, half=half, bk=bk, hs=hs):
                            ins = None
                            for k in range(8):
                                ins = e.matmul(bk, lhsT=hT[hs][:, k, s_ * 128:(s_ + 1) * 128],
                                               rhs=W[:, k, 2048 + half * 512:2048 + (half + 1) * 512],
                                               start=(k == 0), stop=(k == 7))
                            return ins
                        K.op("pe", mmv, reads=[b_w[4 + half], b_hT[hs]], writes=[bb])
                        if half == 0:
                            K.op("act", lambda e, vs=vs, bk=bk: e.activation(
                                out=vs[:, 0:512], in_=bk, func=AF.Copy), reads=[bb], writes=[bv])
                        else:
                            K.op("dve", lambda e, vs=vs, bk=bk: e.tensor_copy(
                                out=vs[:, 512:1024], in_=bk), reads=[bb], writes=[bv])
                    r0 = t * TT + s_ * 128
                    K.dma("sp", vv[r0:r0 + 128, :], vs, reads=[bv])
                if t + 1 < NT:
                    nrm.transpose((t + 1) % 2)

        def attn_phase():
            cv.reset(PERSIST)
            q2 = [cv.take(BF16, [S]) for _ in range(2)]
            k2 = [cv.take(BF16, [S]) for _ in range(2)]
            b_q2 = [Buf(), Buf()]; b_k2 = [Buf(), Buf()]
            Vb = [cv.take(BF16, [48, 132]) for _ in range(2)]
            b_Vb = [Buf(), Buf()]
            Ena = [cv.take(BF16, [2, 21, 128]) for _ in range(2)]
            b_Ena = [Buf(), Buf()]
            Edl = [cv.take(BF16, [3, 2, 384]) for _ in range(2)]
            b_Edl = [Buf(), Buf()]
            stE = cv.take(F32, [21 * 128]); b_stE = Buf()
            mna = cv.take(F32, [21 * 128]); b_mna = Buf()
            mdl = cv.take(F32, [384]); b_mdl = Buf()
            exS = [cv.take(BF16, [3, 4, 128]) for _ in range(2)]
            b_exS = [Buf(), Buf()]
            PT = [cv.take(BF16, [3, 4, 128]) for _ in range(3)]
            b_PT = [Buf() for _ in range(3)]
            ostg = [cv.take(F32, [8, 132]) for _ in range(2)]
            b_ostg = [Buf(), Buf()]
            ST = [psum[:, 0:1536].rearrange("p (a b c) -> p a b c", a=3, b=4),
                  psum[:, 1536:3072].rearrange("p (a b c) -> p a b c", a=3, b=4)]
            b_ST = [Buf(), Buf()]
            Ob = [bank(6), bank(7)]
            b_Ob = [Buf(), Buf()]

            K.dma("sp", mna, nam.rearrange("p a b -> p (a b)"), writes=[b_mna])
            K.dma("sp", mdl, dlm, writes=[b_mdl])
            for sl in range(2):
                K.op("pool", lambda e, sl=sl: e.memset(Vb[sl].rearrange("p a b -> p (a b)"), 0.0),
                     writes=[b_Vb[sl]])
                K.op("pool", lambda e, sl=sl: e.memset(Vb[sl][:, :, 1:2], 1.0), writes=[b_Vb[sl]])
                K.op("pool", lambda e, sl=sl: e.memset(Vb[sl][:, :, 130:131], 1.0), writes=[b_Vb[sl]])

            blk_n = [0]

            def run_blocks(blocks, psl):
                def emit_qk(bl):
                    i = bl["i"]
                    st = ST[i % 2]

                    def mm(e, bl=bl, st=st):
                        ins = None
                        for (hh, ci, kap, qap, nk) in bl["qk"]:
                            dst = st[0:nk, hh, ci, :] if ci < 4 else st[0:nk, 2, hh, :]
                            ins = e.matmul(dst, lhsT=kap, rhs=qap, start=True, stop=True)
                        return ins
                    K.op("pe", mm, reads=[b_q2[psl], b_k2[psl]], writes=[b_ST[i % 2]])

                if DBG["nblocks"] is not None:
                    blocks = blocks[:DBG["nblocks"]]
                for bi, bl in enumerate(blocks):
                    bl["i"] = blk_n[0] + bi
                if blocks and DBG["stage"] >= 1:
                    emit_qk(blocks[0])
                for bi, bl in enumerate(blocks):
                    i = bl["i"]
                    st, ex, pt = ST[i % 2], exS[i % 2], PT[i % 3]
                    for sel in (bl["exp"] if DBG["stage"] >= 2 else []):
                        K.op("act", lambda e, sel=sel, st=st, ex=ex: e.activation(
                            out=sel(ex), in_=sel(st), func=AF.Exp),
                            reads=[b_ST[i % 2]], writes=[b_exS[i % 2]])
                    for (sel, eap) in (bl["mul"] if DBG["stage"] >= 3 else []):
                        K.op("dve", lambda e, sel=sel, eap=eap, ex=ex, pt=pt: e.tensor_tensor(
                            out=sel(pt), in0=sel(ex), in1=eap, op=ALU.mult),
                            reads=[b_exS[i % 2], bl["ebuf"]], writes=[b_PT[i % 3]])
                    if bi + 1 < len(blocks) and DBG["stage"] >= 1:
                        emit_qk(blocks[bi + 1])
                    ob = Ob[i % 2]
                    if DBG["stage"] < 4:
                        continue

                    def pv(e, bl=bl, pt=pt, ob=ob):
                        ins = None
                        for hh in range(2):
                            lst = [p_ for p_ in bl["pv"] if p_[0] == hh]
                            for n_, (hh_, ci, vap, nk) in enumerate(lst):
                                lhs = pt[0:nk, hh, ci, :] if ci < 4 else pt[0:nk, 2, hh, :]
                                ins = e.matmul(ob[:, hh * 66:(hh + 1) * 66], lhsT=lhs, rhs=vap,
                                               start=(n_ == 0), stop=(n_ == len(lst) - 1))
                        return ins
                    K.op("pe", pv, reads=[b_PT[i % 3], b_Vb[psl]], writes=[b_Ob[i % 2]])
                    osl, on, odst = bl["out"]
                    if DBG["stage"] < 5:
                        continue
                    K.op("act", lambda e, ob=ob, osl=osl, on=on: e.activation(
                        out=ostg[osl][:, on, :], in_=ob[:, 0:132], func=AF.Copy),
                        reads=[b_Ob[i % 2]], writes=[b_ostg[osl]])
                    if odst is not None:
                        K.dma("sp", odst[0], ostg[osl][:, 0:odst[1], :], reads=[b_ostg[osl]])
                blk_n[0] += len(blocks)

            ost_n = [0]
            for pair in (range(8) if b_pairs is None else b_pairs):
                psl = pair % 2
                K.dma("sp", q2[psl], qT[pair * 128:(pair + 1) * 128, :], writes=[b_q2[psl]])
                K.dma("sp", k2[psl], kT[pair * 128:(pair + 1) * 128, :], writes=[b_k2[psl]])
                c0 = pair * 128
                if pair < 4:
                    for hh in range(2):
                        h = pair * 2 + hh
                        K.dma("sp", stE, nab[:, h].rearrange("p a b -> p (a b)"), writes=[b_stE])
                        K.op("dve", lambda e: e.tensor_tensor(out=stE, in0=stE, in1=mna, op=ALU.add),
                             reads=[b_mna], writes=[b_stE])
                        K.op("act", lambda e, hh=hh, psl=psl: e.activation(
                            out=Ena[psl][:, hh].rearrange("p a b -> p (a b)"), in_=stE, func=AF.Exp),
                            reads=[b_stE], writes=[b_Ena[psl]])
                    vsrc = vv.rearrange("(j i) c -> i j c", i=128)
                    for g in range(4):
                        K.dma("sp", Vb[psl][:, g * 8:(g + 1) * 8, 2:130],
                              vsrc[:, g * 8:(g + 1) * 8, c0:c0 + 128], writes=[b_Vb[psl]])
                    blocks = []
                    for m in range(32):
                        chunks = _na_chunks(m)
                        idx0 = _na_case_index(m, chunks[0])
                        bl = dict(qk=[], exp=[], mul=[], pv=[], ebuf=b_Ena[psl])
                        for ci, j in enumerate(chunks):
                            for hh in range(2):
                                kap = k2[psl][hh * 64:(hh + 1) * 64, j * 128:(j + 1) * 128]
                                qap = q2[psl][hh * 64:(hh + 1) * 64, m * 128:(m + 1) * 128]
                                bl["qk"].append((hh, ci, kap, qap, 128))
                                bl["pv"].append((hh, ci, Vb[psl][:, j, hh * 66:(hh + 1) * 66], 128))
                        if len(chunks) == 5:
                            d_ = {(q_[0], q_[1]): q_ for q_ in bl["qk"]}
                            order = [(0, 4)] + [(h_, c_) for c_ in range(4) for h_ in (1, 0)] + [(1, 4)]
                            bl["qk"] = [d_[o_] for o_ in order]
                        nj = min(len(chunks), 4)
                        sel1 = (lambda t_, nj=nj: t_[:, 0:2, 0:nj, :])
                        bl["exp"].append(sel1)
                        bl["mul"].append((sel1, Ena[psl][:, :, idx0:idx0 + nj, :]))
                        if len(chunks) == 5:
                            sel2 = (lambda t_: t_[:, 2, 0:2, :])
                            bl["exp"].append(sel2)
                            bl["mul"].append((sel2, Ena[psl][:, :, idx0 + 4, :]))
                        g8 = m // 8
                        if m % 8 == 0:
                            osl = ost_n[0] % 2
                            ost_n[0] += 1
                        odst = None
                        if m % 8 == 7:
                            dd = oacc[0].rearrange("(n q) c -> q n c", q=128)[:, g8 * 8:(g8 + 1) * 8,
                                                                                 pair * 132:(pair + 1) * 132]
                            odst = (dd, 8)
                        bl["out"] = (osl, m % 8, odst)
                        blocks.append(bl)
                    run_blocks(blocks, psl)
                else:
                    dp = pair - 4
                    for hh in range(2):
                        h = dp * 2 + hh
                        K.dma("sp", stE[:, 0:3 * 384], dlb[:, h].rearrange("p a b -> p (a b)"),
                              writes=[b_stE])
                        K.op("dve", lambda e: e.tensor_tensor(
                            out=stE[:, 0:3 * 384].rearrange("p (a b) -> p a b", a=3),
                            in0=stE[:, 0:3 * 384].rearrange("p (a b) -> p a b", a=3),
                            in1=mdl.unsqueeze(1).broadcast_to([128, 3, 384]), op=ALU.add),
                            reads=[b_mdl], writes=[b_stE])
                        K.op("act", lambda e, hh=hh, psl=psl: e.activation(
                            out=Edl[psl][:, :, hh, :],
                            in_=stE[:, 0:3 * 384].rearrange("p (a b) -> p a b", a=3), func=AF.Exp),
                            reads=[b_stE], writes=[b_Edl[psl]])
                    for pi, (_, d) in enumerate(DIL):
                        L = S // d
                        nb = L // 128
                        vsrc = vv.rearrange("(u w d) c -> w d u c", w=64, d=d)
                        vdst = Vb[psl][:, 0:d * (nb + 1), :].rearrange("p (r c) e -> p r c e", c=nb + 1)
                        for r in range(d):
                            K.dma("sp", vdst[0:64, r, 1:nb + 1, 2:130],
                                  vsrc[:, r, 1:2 * nb:2, c0:c0 + 128], writes=[b_Vb[psl]])
                            if nb > 1:
                                K.dma("sp", vdst[64:128, r, 1:nb, 2:130],
                                      vsrc[:, r, 2:2 * nb:2, c0:c0 + 128], writes=[b_Vb[psl]])
                            K.dma("sp", vdst[0:64, r, 0, 2:130], vsrc[:, r, 0, c0:c0 + 128],
                                  writes=[b_Vb[psl]])
                        blocks = []
                        E = Edl[psl]
                        for r in range(d):
                            ngrp = (nb + 7) // 8
                            for b in range(nb):
                                bl = dict(qk=[], exp=[], mul=[], pv=[], ebuf=b_Edl[psl])
                                lo = (0, b, 0 if b == 0 else 128 * b - 64, 64 if b == 0 else 128,
                                      (256, 384) if b == 0 else (0, 128))
                                hi = (1, b + 1, 128 * (b + 1) - 64, 64 if b + 1 == nb else 128, (128, 256))
                                q0 = r + d * 128 * b
                                for (ci, cidx, p0, nk, ec) in (lo, hi):
                                    k0 = r + d * p0
                                    for hh in range(2):
                                        kap = k2[psl][hh * 64:(hh + 1) * 64, k0:k0 + d * (nk - 1) + 1:d]
                                        qap = q2[psl][hh * 64:(hh + 1) * 64, q0:q0 + d * 127 + 1:d]
                                        bl["qk"].append((hh, ci, kap, qap, nk))
                                        bl["pv"].append((hh, ci,
                                                         Vb[psl][0:nk, r * (nb + 1) + cidx, hh * 66:(hh + 1) * 66],
                                                         nk))
                                if lo[3] == 128 and hi[3] == 128:
                                    sel = (lambda t_: t_[:, 0:2, 0:2, :])
                                    bl["exp"].append(sel)
                                    bl["mul"].append((sel, E[:, pi, :, 0:256].rearrange(
                                        "p h (c q) -> p h c q", q=128)))
                                else:
                                    for (ci, cidx, p0, nk, ec) in (lo, hi):
                                        sel = (lambda t_, ci=ci, nk=nk: t_[0:nk, 0:2, ci, :])
                                        bl["exp"].append(sel)
                                        bl["mul"].append((sel, E[0:nk, pi, :, ec[0]:ec[1]]))
                                bn = b % 8
                                if bn == 0:
                                    osl = ost_n[0] % 2
                                    ost_n[0] += 1
                                odst = None
                                if bn == 7 or b == nb - 1:
                                    g8 = b // 8
                                    cnt = bn + 1
                                    dd = oacc[1 + pi].rearrange("(n q d) c -> d q n c", q=128, d=d)[
                                        r, :, g8 * 8:g8 * 8 + cnt, dp * 132:(dp + 1) * 132]
                                    odst = (dd, cnt)
                                bl["out"] = (osl, bn, odst)
                                blocks.append(bl)
                        run_blocks(blocks, psl)

        def outproj_phase():
            cv.reset(PERSIST)
            W = cv.take(BF16, [8, D]); b_w = Buf()
            gout = cv.take(F32, [D]); b_gout = Buf()
            gpost = cv.take(F32, [D]); b_gpost = Buf()
            oin = [cv.take(F32, [4, 528]) for _ in range(2)]
            b_oin = [Buf(), Buf()]
            ocat = [cv.take(F32, [D]) for _ in range(2)]
            b_ocat = [Buf(), Buf()]
            obf = [cv.take(BF16, [D]) for _ in range(2)]
            b_obf = [Buf(), Buf()]
            oT = [cv.take(BF16, [8, TT]) for _ in range(2)]
            b_oT = [Buf(), Buf()]
            tpb = [bank(0, 2), bank(2, 2)]
            b_tp = [Buf(), Buf()]
            dpair = [bank(4, 2), bank(6, 2)]
            b_dp = [Buf(), Buf()]
            load_gain(3, gout, b_gout)
            load_gain(4, gpost, b_gpost)
            K.dma("pool", W, wo.rearrange("(k p) f -> p k f", p=128), writes=[b_w])
            post = Post(x1, x2, gpost, b_gpost, False)
            osrc = oacc.rearrange("k s f -> s k f")
            n = [0]

            def prep(t):
                hs = t % 2
                for s_ in range(4):
                    i = n[0]; n[0] += 1
                    sl = i % 2
                    r0 = t * TT + s_ * 128
                    o_ = oin[sl]
                    K.dma("sp", o_, osrc[r0:r0 + 128, :, :], writes=[b_oin[sl]])

                    K.op("pool", lambda e, o_=o_: e.tensor_tensor(
                        out=o_[:, 1, :], in0=o_[:, 1, :], in1=o_[:, 2, :], op=ALU.add),
                        writes=[b_oin[sl]])
                    K.op("pool", lambda e, o_=o_: e.tensor_tensor(
                        out=o_[:, 1, :], in0=o_[:, 1, :], in1=o_[:, 3, :], op=ALU.add),
                        writes=[b_oin[sl]])
                    c, sb = newstat(16)
                    ov = o_[:, 0:2, :].rearrange("p g (a e) -> p g a e", e=132)
                    rc0 = stat[:, c:c + 8].rearrange("p (g a) -> p g a", g=2)
                    rc1 = stat[:, c + 8:c + 16].rearrange("p (g a) -> p g a", g=2)

                    def rcp(e, ov=ov, rc0=rc0, rc1=rc1):
                        e.reciprocal(out=rc0, in_=ov[:, :, :, 1])
                        return e.reciprocal(out=rc1, in_=ov[:, :, :, 130])
                    K.op("dve", rcp, reads=[b_oin[sl]], writes=sb)
                    oc = ocat[sl]
                    ocv = oc.rearrange("p (g a h e) -> p g a h e", g=2, a=4, h=2)

                    def nrmz(e, ov=ov, ocv=ocv, rc0=rc0, rc1=rc1):
                        for g in range(2):
                            e.tensor_tensor(out=ocv[:, g, :, 0, :], in0=ov[:, g, :, 2:66],
                                            in1=rc0[:, g, :].unsqueeze(2).broadcast_to([128, 4, 64]),
                                            op=ALU.mult)
                            ins = e.tensor_tensor(out=ocv[:, g, :, 1, :], in0=ov[:, g, :, 66:130],
                                                  in1=rc1[:, g, :].unsqueeze(2).broadcast_to([128, 4, 64]),
                                                  op=ALU.mult)
                        return ins
                    K.op("dve", nrmz, reads=[b_oin[sl]] + sb, writes=[b_ocat[sl]])
                    c2, sb2 = newstat(6)
                    ob = obf[sl]

                    def sq(e, oc=oc, ob=ob, c2=c2):
                        e.activation(out=ob[:, 0:512], in_=oc[:, 0:512], func=AF.Square,
                                     accum_out=stat[:, c2:c2 + 1])
                        return e.activation(out=ob[:, 512:1024], in_=oc[:, 512:1024], func=AF.Square,
                                            accum_out=stat[:, c2 + 1:c2 + 2])
                    K.op("act", sq, reads=[b_ocat[sl]], writes=[b_obf[sl]] + sb2[0:2])
                    K.op("act", lambda e, c2=c2: e.activation(
                        out=stat[:, c2 + 2:c2 + 4], in_=stat[:, c2:c2 + 2],
                        func=AF.Sqrt, scale=1.0 / 512, bias=EPS), reads=sb2[0:2], writes=sb2[2:4])
                    K.op("dve", lambda e, c2=c2: e.reciprocal(out=stat[:, c2 + 4:c2 + 6],
                                                              in_=stat[:, c2 + 2:c2 + 4]),
                         reads=sb2[0:4], writes=sb2[4:6])

                    def scl(e, oc=oc, ob=ob, c2=c2):
                        e.scalar_tensor_tensor(out=ob[:, 0:512], in0=oc[:, 0:512],
                                               scalar=stat[:, c2 + 4:c2 + 5], in1=gout[:, 0:512],
                                               op0=ALU.mult, op1=ALU.mult)
                        return e.scalar_tensor_tensor(out=ob[:, 512:1024], in0=oc[:, 512:1024],
                                                      scalar=stat[:, c2 + 5:c2 + 6], in1=gout[:, 512:1024],
                                                      op0=ALU.mult, op1=ALU.mult)
                    K.op("dve", scl, reads=[b_ocat[sl], b_gout] + sb2[4:6], writes=[b_obf[sl]])
                    tp, btp = tpb[s_ % 2], b_tp[s_ % 2]

                    def mm(e, ob=ob, tp=tp):
                        ins = None
                        for k in range(8):
                            ins = e.matmul(tp[:, k * 128:(k + 1) * 128],
                                           lhsT=ob[:, k * 128:(k + 1) * 128], rhs=ident,
                                           start=True, stop=True)
                        return ins
                    K.op("pe", mm, reads=[b_obf[sl], b_ident], writes=[btp])
                    dst = oT[hs][:, :, s_ * 128:(s_ + 1) * 128]
                    K.op("dve", lambda e, dst=dst, tp=tp: e.tensor_copy(
                        out=dst, in_=tp.rearrange("p (k q) -> p k q", q=128)),
                        reads=[btp], writes=[b_oT[hs]])

            prep(0)
            for t in range(NT):
                hs = t % 2
                if t + 1 < NT:
                    prep(t + 1)
                for s_ in range(4):
                    dsl = s_ % 2
                    dps = dpair[dsl]

                    def mmd(e, s_=s_, dps=dps, hs=hs):
                        ins = None
                        for half in range(2):
                            for k in range(8):
                                ins = e.matmul(dps[:, half * 512:(half + 1) * 512],
                                               lhsT=oT[hs][:, k, s_ * 128:(s_ + 1) * 128],
                                               rhs=W[:, k, half * 512:(half + 1) * 512],
                                               start=(k == 0), stop=(k == 7))
                        return ins
                    K.op("pe", mmd, reads=[b_oT[hs], b_w], writes=[b_dp[dsl]])
                    post.run(t * TT + s_ * 128, dps, b_dp[dsl])

        phases = [("A1", lambda: ffn_phase(x_in, x1, w1g, w1u, w1d, 0, 1, False)),
                  ("A2", qkv_phase),
                  ("B", attn_phase),
                  ("C1", outproj_phase),
                  ("C2", lambda: ffn_phase(x2, out, w2g, w2u, w2d, 5, 6, True))]
        for name, fn in phases:
            if only is not None and name != only:
                continue
            fn()
            K.barrier()
            if stop_after == name:
                break
        K.finish()
        K.emit(block)
    return nc


def _prep_inputs(x, ffn1_pre_g, ffn1_w_gate, ffn1_w_up, ffn1_w_down, ffn1_post_g,
                 mix_pre_g, w_qkv, na_rel_bias, t5_rel_bias, na_out_g, dil_out_g, w_out,
                 mix_post_g, ffn2_pre_g, ffn2_w_gate, ffn2_w_up, ffn2_w_down, ffn2_post_g):
    f = lambda a: np.ascontiguousarray(np.asarray(a, dtype=np.float32))
    gall = np.zeros((8, D), np.float32)
    gall[0] = f(ffn1_pre_g)[0]
    gall[1] = f(ffn1_post_g)[0]
    gall[2] = f(mix_pre_g)[0]
    gall[3, :512] = f(na_out_g)[0]
    gall[3, 512:] = f(dil_out_g)[0]
    gall[4] = f(mix_post_g)[0]
    gall[5] = f(ffn2_pre_g)[0]
    gall[6] = f(ffn2_post_g)[0]
    dr, dc, valid = _na_tables()
    nb_ = f(na_rel_bias)[0]
    nab = nb_[:, dr, dc]
    nab = np.ascontiguousarray(nab.transpose(2, 0, 1, 3))
    nam = np.ascontiguousarray(np.where(valid, 0.0, NEG).astype(np.float32).transpose(1, 0, 2))
    bucket, dvalid = _dil_tables()
    t5 = f(t5_rel_bias)
    dlb = t5[:, bucket]
    dlb = np.ascontiguousarray(dlb.transpose(2, 0, 1, 3))
    dlm = np.where(dvalid, 0.0, NEG).astype(np.float32)
    common = dict(gall=gall, w1g=f(ffn1_w_gate)[0], w1u=f(ffn1_w_up)[0], w1d=f(ffn1_w_down)[0],
                  w2g=f(ffn2_w_gate)[0], w2u=f(ffn2_w_up)[0], w2d=f(ffn2_w_down)[0],
                  wqkv=f(w_qkv)[0], wo=f(w_out)[0], nab=nab, nam=nam, dlb=dlb, dlm=dlm)
    xs = f(x)
    return [dict(common, x=np.ascontiguousarray(xs[b])) for b in range(xs.shape[0])]


def kernel(**inputs):
    in_maps = _prep_inputs(**inputs)
    nc = build_program()
    res = run_bass_kernel_spmd(nc, in_maps, core_ids=list(range(len(in_maps))))
    return np.stack([np.asarray(r["out"], dtype=np.float32) for r in res.results], axis=0)
```
